# Optimizing a Trainium2 kernel written in Bass

```python
import math
import jax, jax.numpy as jnp
from jax import lax
import numpy as np

D_MODEL = 1024
BATCH = 8
SEQ = 2048
DEPTH = 4

GRID_W = 64
MIX_WIDTH = 2 * D_MODEL
HG_WIDTH = MIX_WIDTH // 2
HG_HEAD_DIM = 128
HG_HEADS = HG_WIDTH // HG_HEAD_DIM
HG_CHUNK = 32
NA_WIDTH = MIX_WIDTH - HG_WIDTH
NA_HEAD_DIM = 64
NA_HEADS = NA_WIDTH // NA_HEAD_DIM
NA_ROWS_MAX = 8
NA_KC = 16
NA_SPAN = 2 * NA_KC
IN_COLS = 5 * HG_WIDTH + 4 * NA_WIDTH
ADA_COLS = 3 * D_MODEL
DEEPNORM_ALPHA = (2 * DEPTH) ** 0.25
DEEPNORM_BETA = (8 * DEPTH) ** -0.25
LN_EPS = 1e-5
RMS_EPS = 1e-6

kernel_name = "hymba_hgrn2_natten_deepnorm_encoder"


def _layernorm(x):
    xf = x.astype(jnp.float32)
    mu = jnp.mean(xf, axis=-1, keepdims=True)
    var = jnp.mean(jnp.square(xf - mu), axis=-1, keepdims=True)
    return (xf - mu) * lax.rsqrt(var + LN_EPS)


def _rmsnorm(x, g):
    xf = x.astype(jnp.float32)
    return xf * lax.rsqrt(jnp.mean(jnp.square(xf), axis=-1, keepdims=True) + RMS_EPS) * g.astype(jnp.float32)


def _hgrn2_lower_bounds(lb_logits):
    p = jax.nn.softmax(lb_logits.astype(jnp.float32), axis=0)
    cs = jnp.cumsum(p, axis=0)
    return cs - cs[0:1]


def _hgrn2_chunk_scan(q, k, v, log_f):
    B, H, S, DK = q.shape
    DV = v.shape[-1]
    C = HG_CHUNK
    N = S // C
    q, k, log_f = (t.reshape(B, H, N, C, DK) for t in (q, k, log_f))
    v = v.reshape(B, H, N, C, DV)
    b = jnp.cumsum(log_f, axis=3)
    b_mid = b[:, :, :, C // 2 - 1:C // 2]
    b_last = b[:, :, :, C - 1:]
    scores = jnp.einsum('bhntd,bhnsd->bhnts', q * jnp.exp(b - b_mid), k * jnp.exp(b_mid - b))
    lower_tri = jnp.tril(jnp.ones((C, C), dtype=bool))
    scores = jnp.where(lower_tri, scores, 0.0)
    o_intra = jnp.einsum('bhnts,bhnsv->bhntv', scores, v)
    q_inter = q * jnp.exp(b)
    k_end = k * jnp.exp(b_last - b)
    decay = jnp.exp(b_last[:, :, :, 0])

    def step(state, xs):
        qc, kc, vc, dc = xs
        o = jnp.einsum('bhtd,bhdv->bhtv', qc, state)
        state = dc[..., None] * state + jnp.einsum('bhtd,bhtv->bhdv', kc, vc)
        return state, o

    xs = tuple(jnp.moveaxis(t, 2, 0) for t in (q_inter, k_end, v, decay))
    _, o_inter = lax.scan(step, jnp.zeros((B, H, DK, DV), q.dtype), xs)
    o = o_intra + jnp.moveaxis(o_inter, 0, 2)
    return o.reshape(B, H, S, DV)


def _hgrn2_branch(q_in, i_in, zf_fw, zf_bw, g_in, lb_fw, lb_bw, norm_g):
    B, S, _ = q_in.shape

    def heads(t):
        return t.astype(jnp.float32).reshape(B, S, HG_HEADS, HG_HEAD_DIM).transpose(0, 2, 1, 3)

    q = heads(q_in)
    v = heads(i_in)

    def direction(z, lb, flip):
        f = heads(lb + (1.0 - lb) * jax.nn.sigmoid(z.astype(jnp.float32)))
        qq, vv = q, v
        if flip:
            f, qq, vv = (jnp.flip(t, axis=2) for t in (f, q, v))
        o = _hgrn2_chunk_scan(qq, 1.0 - f, vv, jnp.log(f))
        return jnp.flip(o, axis=2) if flip else o

    o = direction(zf_fw, lb_fw, False) + direction(zf_bw, lb_bw, True)
    o = _rmsnorm(o.transpose(0, 2, 1, 3), norm_g).reshape(B, S, HG_WIDTH)
    return o.astype(g_in.dtype) * jax.nn.silu(g_in)


def _neighbourhood_attention(q, k, v, rpb):
    B, H, R, W, dh = q.shape
    KR = min(NA_ROWS_MAX, R)
    nqb = W // NA_KC
    qc = jnp.arange(W)
    cs = jnp.clip(qc - NA_KC // 2, 0, W - NA_KC)
    ks = jnp.minimum(cs[::NA_KC], W - NA_SPAN)
    key_cols = ks[:, None] + jnp.arange(NA_SPAN)
    qcols = qc.reshape(nqb, NA_KC)
    cs_b = cs.reshape(nqb, NA_KC)
    kc3 = key_cols[:, None, :]
    col_mask = (kc3 >= cs_b[:, :, None]) & (kc3 < cs_b[:, :, None] + NA_KC)
    col_off = jnp.clip(kc3 - qcols[:, :, None] + NA_KC - 1, 0, 2 * NA_KC - 2)
    col_bias = rpb.astype(jnp.float32)[:, :, col_off]
    scale = NA_HEAD_DIM ** -0.5

    def one_row(r):
        rs = jnp.clip(r - KR // 2, 0, R - KR)
        k_rows = lax.dynamic_slice_in_dim(k, rs, KR, axis=2)
        v_rows = lax.dynamic_slice_in_dim(v, rs, KR, axis=2)
        k_blk = jnp.take(k_rows, key_cols, axis=3)
        v_blk = jnp.take(v_rows, key_cols, axis=3)
        q_row = lax.dynamic_index_in_dim(q, r, axis=2, keepdims=False).reshape(B, H, nqb, NA_KC, dh)
        s = jnp.einsum('bhjqd,bhrjcd->bhjqrc', q_row, k_blk).astype(jnp.float32) * scale
        row_off = rs + jnp.arange(KR) - r + NA_ROWS_MAX - 1
        bias = jnp.take(col_bias, row_off, axis=1).transpose(0, 2, 3, 1, 4)
        s = jnp.where(col_mask[:, :, None, :], s + bias[None], -jnp.inf)
        p = jax.nn.softmax(s.reshape(B, H, nqb, NA_KC, KR * NA_SPAN), axis=-1)
        p = p.reshape(B, H, nqb, NA_KC, KR, NA_SPAN).astype(v.dtype)
        o = jnp.einsum('bhjqrc,bhrjcd->bhjqd', p, v_blk)
        return o.reshape(B, H, W, dh)

    return lax.map(one_row, jnp.arange(R))


def setup_inputs(seed: int = 0) -> dict:
    key = jax.random.key(seed)
    ks = jax.random.split(key, 12)
    f32 = jnp.float32
    x = jax.random.normal(ks[0], (BATCH, SEQ, D_MODEL), f32)
    c = jax.random.normal(ks[1], (BATCH, D_MODEL), f32)
    ada_w = jax.random.normal(ks[2], (DEPTH, D_MODEL, ADA_COLS), f32) * D_MODEL ** -0.5
    ada_b = jax.random.normal(ks[3], (DEPTH, ADA_COLS), f32) * 0.02
    w_in = jax.random.normal(ks[4], (DEPTH, D_MODEL, IN_COLS), f32) * D_MODEL ** -0.5
    lb_logits = jax.random.normal(ks[5], (2, DEPTH, HG_WIDTH), f32)
    hg_norm_g = 1.0 + 0.02 * jax.random.normal(ks[6], (DEPTH, HG_HEAD_DIM), f32)
    rpb = jax.random.normal(ks[7], (DEPTH, NA_HEADS, 2 * NA_ROWS_MAX - 1, 2 * NA_KC - 1), f32) * 0.1
    w_out = jax.random.normal(ks[8], (DEPTH, MIX_WIDTH, D_MODEL), f32) * (MIX_WIDTH ** -0.5 * DEEPNORM_BETA)
    ln_g = 1.0 + 0.02 * jax.random.normal(ks[9], (DEPTH, D_MODEL), f32)
    ln_b = 0.02 * jax.random.normal(ks[10], (DEPTH, D_MODEL), f32)
    return {"x": x, "c": c, "ada_w": ada_w, "ada_b": ada_b, "w_in": w_in, "lb_logits": lb_logits,
            "hg_norm_g": hg_norm_g, "rpb": rpb, "w_out": w_out, "ln_g": ln_g, "ln_b": ln_b}


def reference(x, c, ada_w, ada_b, w_in, lb_logits, hg_norm_g, rpb, w_out, ln_g, ln_b):
    B, S, D = x.shape
    R = S // GRID_W
    lb_fw_all = _hgrn2_lower_bounds(lb_logits[0])
    lb_bw_all = _hgrn2_lower_bounds(lb_logits[1])
    c_act = jax.nn.silu(c)
    split_pts = np.cumsum([HG_WIDTH] * 5 + [NA_WIDTH] * 3).tolist()

    def na_heads(t):
        return t.reshape(B, R, GRID_W, NA_HEADS, NA_HEAD_DIM).transpose(0, 3, 1, 2, 4)

    for l in range(DEPTH):
        mod = c_act @ ada_w[l] + ada_b[l]
        shift, scale, gate = jnp.split(mod, 3, axis=-1)
        h = (_layernorm(x) * (1.0 + scale[:, None, :]) + shift[:, None, :]).astype(x.dtype)
        proj = h @ w_in[l]
        q_h, i_h, zf_fw, zf_bw, g_h, q_n, k_n, v_n, g_n = jnp.split(proj, split_pts, axis=-1)
        y_hg = _hgrn2_branch(q_h, i_h, zf_fw, zf_bw, g_h, lb_fw_all[l], lb_bw_all[l], hg_norm_g[l])
        o_na = _neighbourhood_attention(na_heads(q_n), na_heads(k_n), na_heads(v_n), rpb[l])
        o_na = o_na.transpose(1, 0, 3, 2, 4).reshape(B, S, NA_WIDTH)
        y_na = o_na * jax.nn.silu(g_n)
        y = jnp.concatenate([y_hg, y_na], axis=-1) @ w_out[l]
        z = DEEPNORM_ALPHA * x + gate[:, None, :] * y
        x = (_layernorm(z) * ln_g[l] + ln_b[l]).astype(x.dtype)
    return x
```

```python
import numpy as np
from contextlib import ExitStack
import concourse.bass as bass
import concourse.mybir as mybir
from concourse.bass_utils import run_bass_kernel_spmd

F32 = mybir.dt.float32
BF16 = mybir.dt.bfloat16
U32 = mybir.dt.uint32
AF = mybir.ActivationFunctionType
ALU = mybir.AluOpType

DEPTH = 4
ALPHA = (2 * DEPTH) ** 0.25
LN_EPS = 1e-5
RMS_EPS = 1e-6
NEG = -30000.0


class Buf:
    __slots__ = ("name", "w", "rs", "excl")

    def __init__(self, name):
        self.name = name
        self.excl = False
        self.w = []
        self.rs = []


class Eng:
    def __init__(self, name, eng, sem):
        self.name, self.eng, self.sem = name, eng, sem
        self.cnt = 0
        self.seen = {}


class Sched:
    NDMA = 10

    def __init__(self, nc, stack):
        self.nc = nc
        self.stack = stack
        self.E = {}
        for name, eng in (("pe", nc.tensor), ("act", nc.scalar), ("dve", nc.vector),
                          ("pool", nc.gpsimd), ("sp", nc.sync)):
            sem = stack.enter_context(nc.semaphore("sem_" + name))
            self.E[name] = Eng(name, eng, sem)
        self.nbuf = 0
        self.semkey = {}
        self.dq = {}
        for q in ("sp", "pool"):
            sems = [stack.enter_context(nc.semaphore(f"dsem_{q}{i}")) for i in range(self.NDMA)]
            self.dq[q] = {"sems": sems, "cnt": [0] * self.NDMA, "next": 0}

    def buf(self, name=None):
        self.nbuf += 1
        return Buf(name or f"b{self.nbuf}")

    def _key(self, sem):
        k = id(sem)
        self.semkey[k] = sem
        return k

    def _wait(self, e, deps):
        need = {}
        for d in deps:
            if d is None:
                continue
            sem, val = d
            k = self._key(sem)
            if val > need.get(k, 0):
                need[k] = val
        for k, val in need.items():
            if e.name == "pe" and self.semkey[k] is e.sem:
                continue
            if e.seen.get(k, 0) >= val:
                continue
            e.eng.wait_ge(self.semkey[k], val)
            e.seen[k] = val

    def _deps(self, reads, writes):
        deps = []
        for b in reads:
            deps.extend(b.w)
        for b in writes:
            deps.extend(b.w)
            deps.extend(b.rs)
        return deps

    def op(self, ename, fn, reads=(), writes=()):
        e = self.E[ename]
        if any(b.excl for b in reads):
            writes = list(writes) + [b for b in reads if b.excl]
            reads = [b for b in reads if not b.excl]
        self._wait(e, self._deps(reads, writes))
        inst = fn()
        e.cnt += 1
        inst.then_inc(e.sem, 1)
        ev = (e.sem, e.cnt)
        for b in reads:
            b.rs.append(ev)
        for b in writes:
            b.w = [ev]
            b.rs = []
        return inst

    def _dma(self, qname, o, i):
        e = self.E[qname]
        q = self.dq[qname]
        k = q["next"]
        q["next"] = (k + 1) % self.NDMA
        sem = q["sems"][k]
        if q["cnt"][k]:
            self._wait(e, [(sem, q["cnt"][k])])
        e.eng.dma_start(out=o, in_=i).then_inc(sem, 16)
        q["cnt"][k] += 16
        return (sem, q["cnt"][k])

    def dma_load(self, qname, dst_buf, pairs, reads=()):
        e = self.E[qname]
        self._wait(e, self._deps(reads, [dst_buf]))
        evs = [self._dma(qname, o, i) for (o, i) in pairs]
        for b in reads:
            b.rs.extend(evs)
        dst_buf.w = evs
        dst_buf.rs = []

    def dma_store(self, qname, src_bufs, pairs):
        e = self.E[qname]
        self._wait(e, self._deps(src_bufs, []))
        evs = [self._dma(qname, o, i) for (o, i) in pairs]
        for b in src_bufs:
            b.rs.extend(evs)

    def barrier(self):
        evs = [(o.sem, o.cnt) for o in self.E.values() if o.cnt]
        for q in self.dq.values():
            evs += [(s, c) for s, c in zip(q["sems"], q["cnt"]) if c]
        for e in self.E.values():
            self._wait(e, evs)

    def finish(self):
        self.barrier()


def na_tables():
    def rs(r):
        return min(max(r - 4, 0), 24)

    keys = {}
    mlist = []
    for a in range(16):
        lst = []
        for m in range(16):
            key = []
            anyv = False
            for rl in (0, 1):
                for krl in (0, 1):
                    kr, r = 2 * m + krl, 2 * a + rl
                    if rs(r) <= kr <= rs(r) + 7:
                        key.append(kr - r + 7)
                        anyv = True
                    else:
                        key.append(-1)
            if anyv:
                key = tuple(key)
                if key not in keys:
                    keys[key] = len(keys)
                lst.append((m, keys[key]))
        mlist.append(lst)
    return keys, mlist


NA_KEYS, NA_MLIST = na_tables()
NK = len(NA_KEYS)


def host_na_bias(rpb):
    L, H = rpb.shape[0], rpb.shape[1]
    kc = np.arange(64)[:, None]
    qc = np.arange(64)[None, :]
    cs = np.clip(qc - 8, 0, 48)
    cval = (kc >= cs) & (kc < cs + 16)
    coff = np.clip(kc - qc + 15, 0, 30)
    out = np.full((L, H, 2, 64, NK, 2, 64), NEG, np.float32)
    for key, ti in NA_KEYS.items():
        for rl in (0, 1):
            for krl in (0, 1):
                dr = key[rl * 2 + krl]
                if dr < 0:
                    continue
                g = rpb[:, :, dr, :][:, :, coff]
                out[:, :, krl, :, ti, rl, :] = np.where(cval[None, None], g, np.float32(NEG))
    return np.ascontiguousarray(out.reshape(L, H, 128, NK * 128))


def build(NL=DEPTH, debug=False):
    nc = bass.Bass("TRN2", target_bir_lowering=False)

    def din(name, shape, dt=F32):
        return nc.dram_tensor(name, list(shape), dt, kind="ExternalInput").ap()

    x_d = din("x", [2048, 1024])
    c_d = din("c", [8, 128])
    adaw_d = din("ada_w", [DEPTH, 1024, 3072])
    adab_d = din("ada_b", [96, 128])
    win_d = din("w_in", [DEPTH, 1024, 9216])
    lbl_d = din("lbl", [64, 128])
    gn_d = din("gn", [DEPTH, 128])
    nab_d = din("nab", [DEPTH, 16, 128, NK * 128])
    wout_d = din("w_out", [DEPTH, 2048, 1024])
    lng_d = din("ln_g", [DEPTH, 1024])
    lnb_d = din("ln_b", [DEPTH, 1024])
    ident_d = din("ident", [128, 128])
    smask_d = din("smask", [128, 1024])
    tri_d = din("tri", [2, 128, 128], U32)
    y_d = nc.dram_tensor("y", [2048, 1024], F32, kind="ExternalOutput").ap()

    with ExitStack() as st:
        S = Sched(nc, st)

        def sb(name, shape, dt):
            return st.enter_context(nc.sbuf_tensor("s_" + name, list(shape), dt))

        x_sb = sb("x_sb", [128, 16, 1024], F32)
        hT = sb("hT", [128, 8, 2048], BF16)
        yT = sb("yT", [128, 4, 2048], BF16)
        wbuf = [sb(f"wbuf{i}", [128, 2560], F32) for i in range(2)]
        wo_sb = sb("wo_sb", [128, 4, 1024], BF16)
        gate_bc = sb("gate_bc", [128, 1024], F32)
        arena = sb("arena", [128, 13312], F32)
        ident_f = sb("ident_f", [128, 128], F32)
        ident_b = sb("ident_b", [128, 128], BF16)
        smask = sb("smask", [128, 1024], BF16)
        tri = sb("tri", [128, 2, 128], U32)
        ones_f = sb("ones_f", [128, 128], F32)
        cT = sb("cT", [128, 8], F32)
        c8 = sb("c8", [8, 128], F32)
        lbl64 = sb("lbl64", [64, 128], F32)
        lbT = sb("lbT", [128, 2, 4, 8], F32)
        lbtmp = sb("lbtmp", [128, 2, 4, 8], F32)
        lbsum = sb("lbsum", [128, 2, 8], F32)
        coefA = sb("coefA", [128, 2, 4, 8], F32)
        coefB = sb("coefB", [128, 2, 4, 8], F32)
        coefC = sb("coefC", [128, 2, 4, 8], F32)
        gnh_col = sb("gnh_col", [128, 1], F32)
        adab96 = sb("adab96", [96, 128], F32)
        adabT = sb("adabT", [128, DEPTH, 24], F32)
        modT = sb("modT", [128, DEPTH, 24], F32)
        stats = sb("stats", [128, 16, 2, 6], F32)
        mv = sb("mv", [128, 16, 2], F32)
        rstd = sb("rstd", [128, 16], F32)
        small = sb("small", [128, 64], F32)
        ps = st.enter_context(nc.psum_tensor("ps", [128, 8, 512], F32))

        B = {}

        def bf(name):
            if name not in B:
                B[name] = S.buf(name)
            return B[name]

        PS = [bf(f"ps{i}") for i in range(8)]
        for _b in PS:
            _b.excl = True
        rot = {}

        def nxt(group, banks):
            i = rot.get(group, 0)
            rot[group] = i + 1
            return banks[i % len(banks)]

        def af(a, n):
            return arena[:, a:a + n]

        def ab(a, n):
            return arena[:, a:a + n].bitcast(BF16)

        q_sb = af(0, 2048)
        T_sb = af(2048, 1024)
        Lb_sb = af(3072, 1040)
        E_sb = af(4112, 1024)
        qa = [ab(5136, 1024), ab(6160, 1024)]
        ka_sb = ab(7184, 1024)
        kaT = ab(8208, 1024).rearrange("p (j d) -> p j d", d=128)
        qaz = ab(3072, 512).rearrange("p (j t) -> p j t", t=64)
        kaTz = ab(2048, 1024).rearrange("p (j d) -> p j d", d=128)
        v_sb = ab(9232, 1024).rearrange("p (j d) -> p j d", d=128)
        kb_sb = ab(10256, 1024)
        kab = [ka_sb, kb_sb]
        gsg = ka_sb
        QT = ab(0, 1024)
        KT = ab(1024, 1024)
        Vaug = ab(2048, 1040).rearrange("p (j h e) -> p j h e", h=2, e=65)
        gsn = ab(3088, 1024).rearrange("p (j d) -> p j d", d=128)
        bt = [ab(4112, NK * 64), ab(4112 + NK * 64, NK * 64)]
        PTs = [ab(4112 + 2 * NK * 64, 320), ab(4112 + 2 * NK * 64 + 320, 320)]
        na_end = 4112 + 2 * NK * 64 + 640
        ytm = [arena[:, na_end:na_end + 128], arena[:, na_end + 128:na_end + 256]]
        assert na_end + 256 <= 13312, na_end
        xn4 = arena[:, 0:4096].rearrange("p (j d) -> p j d", d=1024)
        lng_bc = arena[:, 4096:5120]
        lnb_bc = arena[:, 5120:6144]

        o_all = arena[:, 0:2048]
        hg_small = arena[:, 11280:13312]
        AZ = hg_small[:, 0:256].rearrange("p (d z n) -> p d z n", d=2, z=2)
        c_sb = hg_small[:, 256:384].rearrange("p (d n) -> p d n", d=2)
        R_sb = [hg_small[:, 384:512], hg_small[:, 512:640]]
        Sb_sb = [hg_small[:, 640:704].bitcast(BF16), hg_small[:, 704:768].bitcast(BF16)]
        scT = [[hg_small[:, 768 + 64 * (2 * d + i):832 + 64 * (2 * d + i)].bitcast(BF16) for i in range(2)]
               for d in range(2)]
        ytmh = [hg_small[:, 1024 + 128 * i:1152 + 128 * i] for i in range(4)]
        junk = hg_small[:, 1536:1664]
        dg = arena[:, 8192:8320]
        ssq = small[:, 0:16]
        rden = small[:, 16:18]
        rstd_h = small[:, 32:48]

        Bx = [bf(f"x{j}") for j in range(16)]
        BhT = [bf(f"hT{g}") for g in range(4)]
        ByT = [bf(f"yT{s}") for s in range(4)]
        Bw = [bf("w0"), bf("w1")]

        def eng_copy(i):
            return "act" if i % 2 == 0 else "dve"

        S.dma_load("sp", bf("ident_f"), [(ident_f[:], ident_d)])
        S.dma_load("pool", bf("smask"), [(smask[:], smask_d)])
        S.dma_load("sp", bf("tri"), [(tri[:], tri_d.rearrange("d p t -> p d t"))])
        S.dma_load("sp", bf("c8"), [(c8[:], c_d)])
        S.dma_load("sp", bf("lbl64"), [(lbl64[:], lbl_d)])
        S.dma_load("sp", bf("adab96"), [(adab96[:], adab_d)])
        for g in range(4):
            S.dma_load("sp", Bx[4 * g], [(x_sb[:, 4 * g:4 * g + 4, :],
                                          x_d[512 * g:512 * (g + 1), :].rearrange("(j p) d -> p j d", p=128))])
            for j in range(4 * g + 1, 4 * g + 4):
                Bx[j].w = list(Bx[4 * g].w)
        S.op("dve", lambda: nc.vector.tensor_copy(out=ident_b[:], in_=ident_f[:]), [bf("ident_f")], [bf("ident_b")])
        S.op("dve", lambda: nc.vector.memset(ones_f[:], 1.0), [], [bf("ones")])
        S.op("pe", lambda: nc.tensor.transpose(out=ps[:, 2, 0:96], in_=adab96[:], identity=ident_f[0:96, 0:96]),
             [bf("adab96"), bf("ident_f")], [PS[2]])
        S.op("dve", lambda: nc.vector.tensor_copy(out=adabT[:].rearrange("p l i -> p (l i)"), in_=ps[:, 2, 0:96]), [PS[2]], [bf("adabT")])

        S.op("pe", lambda: nc.tensor.transpose(out=ps[:, 0, 0:8], in_=c8[:], identity=ident_f[0:8, 0:8]),
             [bf("c8"), bf("ident_f")], [PS[0]])
        S.op("act", lambda: nc.scalar.activation(out=small[:, 48:56], in_=ps[:, 0, 0:8], func=AF.Tanh, scale=0.5), [PS[0]], [bf("small_c")])
        S.op("dve", lambda: nc.vector.scalar_tensor_tensor(out=cT[:], in0=small[:, 48:56], scalar=1.0, in1=ps[:, 0, 0:8],
                                                           op0=ALU.add, op1=ALU.mult), [bf("small_c"), PS[0]], [bf("cT")])
        S.op("dve", lambda: nc.vector.tensor_scalar(out=cT[:], in0=cT[:], scalar1=0.5, scalar2=None, op0=ALU.mult), [bf("cT")], [bf("cT")])

        S.op("pe", lambda: nc.tensor.transpose(out=ps[:, 1, 0:64], in_=lbl64[:], identity=ident_f[0:64, 0:64]),
             [bf("lbl64"), bf("ident_f")], [PS[1]])
        lbflat = lbtmp[:].rearrange("p d l h -> p (d l h)")
        S.op("act", lambda: nc.scalar.activation(out=lbflat, in_=ps[:, 1, 0:64], func=AF.Exp), [PS[1]], [bf("lbtmp")])
        S.op("dve", lambda: nc.vector.tensor_tensor(out=lbsum[:], in0=lbtmp[:, :, 0, :], in1=lbtmp[:, :, 1, :], op=ALU.add), [bf("lbtmp")], [bf("lbsum")])
        S.op("dve", lambda: nc.vector.tensor_tensor(out=lbsum[:], in0=lbsum[:], in1=lbtmp[:, :, 2, :], op=ALU.add), [bf("lbtmp"), bf("lbsum")], [bf("lbsum")])
        S.op("dve", lambda: nc.vector.tensor_tensor(out=lbsum[:], in0=lbsum[:], in1=lbtmp[:, :, 3, :], op=ALU.add), [bf("lbtmp"), bf("lbsum")], [bf("lbsum")])
        S.op("dve", lambda: nc.vector.reciprocal(out=lbsum[:], in_=lbsum[:]), [bf("lbsum")], [bf("lbsum")])
        S.op("dve", lambda: nc.vector.memset(lbT[:, :, 0, :], 0.0), [], [bf("lbT")])
        for l in range(1, 4):
            S.op("dve", lambda l=l: nc.vector.tensor_tensor(out=lbtmp[:, :, l, :], in0=lbtmp[:, :, l, :], in1=lbsum[:], op=ALU.mult),
                 [bf("lbsum"), bf("lbtmp")], [bf("lbtmp")])
            S.op("dve", lambda l=l: nc.vector.tensor_tensor(out=lbT[:, :, l, :], in0=lbT[:, :, l - 1, :], in1=lbtmp[:, :, l, :], op=ALU.add),
                 [bf("lbtmp"), bf("lbT")], [bf("lbT")])
        S.op("dve", lambda: nc.vector.tensor_scalar(out=coefA[:], in0=lbT[:], scalar1=-0.5, scalar2=0.5, op0=ALU.mult, op1=ALU.add), [bf("lbT")], [bf("coef")])
        S.op("dve", lambda: nc.vector.tensor_scalar(out=coefB[:], in0=lbT[:], scalar1=0.5, scalar2=0.5, op0=ALU.mult, op1=ALU.add), [bf("lbT")], [bf("coef")])
        S.op("dve", lambda: nc.vector.tensor_scalar(out=coefC[:], in0=lbT[:], scalar1=0.5, scalar2=-0.5, op0=ALU.mult, op1=ALU.add), [bf("lbT")], [bf("coef")])

        def emit_mod_group(l, g):
            wi = nxt("w", [0, 1])
            wv = wbuf[wi][:, 0:2048].rearrange("p (k c) -> p k c", c=256)
            S.dma_load("sp", Bw[wi], [(wv, adaw_d[l, :, 256 * g:256 * (g + 1)].rearrange("(k p) c -> p k c", p=128))])
            bank = nxt("fm", [0, 1])
            for cc in range(2):
                for k in range(8):
                    S.op("pe", lambda k=k, cc=cc: nc.tensor.matmul(ps[:, bank, cc:cc + 1], lhsT=wv[:, k, 128 * cc:128 * (cc + 1)], rhs=cT[:, k:k + 1],
                                                                 start=(k == 0), stop=(k == 7)), [Bw[wi], bf("cT")], [PS[bank]])
            S.op("dve", lambda: nc.vector.tensor_tensor(out=modT[:, l, 2 * g:2 * g + 2], in0=ps[:, bank, 0:2], in1=adabT[:, l, 2 * g:2 * g + 2], op=ALU.add),
                 [PS[bank], bf("adabT")], [bf(f"modT{l}")])

        def emit_mod_finish(l):
            S.op("dve", lambda: nc.vector.tensor_scalar(out=modT[:, l, 8:16], in0=modT[:, l, 8:16], scalar1=1.0, scalar2=None, op0=ALU.add),
                 [bf(f"modT{l}")], [bf(f"modT{l}")])

        def emit_gate_bc(l):
            S.dma_load("sp", bf("gnh"), [(gnh_col[:], gn_d[l:l + 1, :].rearrange("o d -> d o"))])
            S.op("dve", lambda: nc.vector.tensor_scalar(out=gnh_col[:], in0=gnh_col[:], scalar1=0.5, scalar2=None, op0=ALU.mult), [bf("gnh")], [bf("gnh")])
            for hh in range(2):
                bank = nxt("fm", [0, 1])
                for jj in range(4):
                    j = 4 * hh + jj
                    S.op("dve", lambda j=j: nc.vector.tensor_scalar(out=dg, in0=ident_f[:], scalar1=modT[:, l, 16 + j:17 + j], scalar2=None, op0=ALU.mult),
                         [bf("ident_f"), bf(f"modT{l}")], [bf("dg")])
                    S.op("pe", lambda jj=jj: nc.tensor.matmul(ps[:, bank, 128 * jj:128 * (jj + 1)], lhsT=ones_f[:], rhs=dg, start=True, stop=True),
                         [bf("dg"), bf("ones")], [PS[bank]])
                S.op("act", lambda hh=hh: nc.scalar.copy(out=gate_bc[:, 512 * hh:512 * (hh + 1)], in_=ps[:, bank, :]), [PS[bank]], [bf("gate_bc")])

        def ln_stats(bufs_extra_w=()):
            for j in range(16):
                for hh in range(2):
                    S.op("dve", lambda j=j, hh=hh: nc.vector.bn_stats(out=stats[:, j, hh, :], in_=x_sb[:, j, 512 * hh:512 * (hh + 1)]),
                         [Bx[j]], [bf("stats")])
                S.op("dve", lambda j=j: nc.vector.bn_aggr(out=mv[:, j, :], in_=stats[:, j, :, :].rearrange("p a b -> p (a b)")),
                     [bf("stats")], [bf("mv")])
            S.op("act", lambda: nc.scalar.activation(out=rstd[:], in_=mv[:, :, 1], func=AF.Ln, bias=eps_ln[:, 0:1], scale=1.0), [bf("mv"), bf("eps")], [bf("rstd")])
            S.op("act", lambda: nc.scalar.activation(out=rstd[:], in_=rstd[:], func=AF.Exp, scale=-0.5), [bf("rstd")], [bf("rstd")])

        eps_ln = sb("eps_ln", [128, 2], F32)
        S.op("dve", lambda: nc.vector.memset(eps_ln[:, 0:1], LN_EPS), [], [bf("eps")])
        S.op("dve", lambda: nc.vector.memset(eps_ln[:, 1:2], RMS_EPS), [bf("eps")], [bf("eps")])

        def emit_ln1(l):
            ln_stats()
            for g in range(4):
                for jj in range(4):
                    j = 4 * g + jj
                    S.op("dve", lambda j=j, jj=jj: nc.vector.tensor_scalar(out=xn4[:, jj, :], in0=x_sb[:, j, :], scalar1=mv[:, j, 0:1], scalar2=rstd[:, j:j + 1],
                                                                         op0=ALU.subtract, op1=ALU.mult), [Bx[j], bf("mv"), bf("rstd")], [bf(f"xn{jj}")])
                for k in range(8):
                    bank = nxt("tm", [2, 3])
                    for jj in range(4):
                        S.op("pe", lambda k=k, jj=jj: nc.tensor.transpose(out=ps[:, bank, 128 * jj:128 * (jj + 1)], in_=xn4[:, jj, 128 * k:128 * (k + 1)], identity=ident_f[:]),
                             [bf(f"xn{jj}"), bf("ident_f")], [PS[bank]])
                    S.op("act", lambda k=k, g=g: nc.scalar.activation(out=hT[:, k, 512 * g:512 * (g + 1)], in_=ps[:, bank, :], func=AF.Identity,
                                                                   scale=modT[:, l, 8 + k:9 + k], bias=modT[:, l, k:k + 1]),
                         [PS[bank], bf(f"modT{l}")], [BhT[g]])

        def load_w_hg(l, hh):
            wi = nxt("w", [0, 1])
            wv = wbuf[wi][:].bitcast(BF16).rearrange("p (k s c) -> p k s c", k=8, s=5)
            src = win_d[l, :, 0:5120].rearrange("(k p) (s h c) -> p k s h c", p=128, s=5, h=8)
            S.dma_load("pool", Bw[wi], [(wv[:, :, si, :], src[:, :, si, hh, :]) for si in range(5)])
            return wi, wv

        def load_w_na(l, p):
            wi = nxt("w", [0, 1])
            wv = wbuf[wi][:, 0:2048].bitcast(BF16).rearrange("p (k s c) -> p k s c", k=8, s=4)
            src = win_d[l, :, 5120:9216].rearrange("(k p) (s h c) -> p k s h c", p=128, s=4, h=8)
            S.dma_load("pool", Bw[wi], [(wv[:, :, si, :], src[:, :, si, p, :]) for si in range(4)])
            return wi, wv

        def fm_proj(wi, wsel, g, evac):
            bank = nxt("fm", [0, 1])
            for k in range(8):
                S.op("pe", lambda k=k: nc.tensor.matmul(ps[:, bank, :], lhsT=wsel(k), rhs=hT[:, k, 512 * g:512 * (g + 1)], start=(k == 0), stop=(k == 7)),
                     [Bw[wi], BhT[g]], [PS[bank]])
            evac(bank)

        def tm_proj(wi, wsel, j, ncol, evac):
            bank = nxt("tm", [2, 3])
            for k in range(8):
                S.op("pe", lambda k=k: nc.tensor.matmul(ps[:, bank, 0:ncol], lhsT=hT[:, k, 128 * j:128 * (j + 1)], rhs=wsel(k), start=(k == 0), stop=(k == 7)),
                     [Bw[wi], BhT[j // 4]], [PS[bank]])
            evac(bank)

        tmpA = ytmh

        def emit_hgrn_head(l, hh, slot):
            wi, W = load_w_hg(l, hh)
            TLE = [bf("T_sb"), bf("Lb"), bf("E")]
            for j in range(16):
                def evac(bank, j=j):
                    S.op("act", lambda: nc.scalar.copy(out=v_sb[:, j, :], in_=ps[:, bank, 0:128]), [PS[bank]], [bf("v_sb")])
                tm_proj(wi, lambda k: W[:, k, 1, :], j, 128, evac)
            ck(3.1)
            for g in range(4):
                def evac(bank, g=g):
                    S.op("act", lambda: nc.scalar.copy(out=q_sb[:, 512 * g:512 * (g + 1)], in_=ps[:, bank, :]), [PS[bank]], [bf("q_sb")])
                fm_proj(wi, lambda k: W[:, k, 0, :], g, evac)
            ck(3.2)
            for d in range(2):
                cA = coefA[:, d, l, hh:hh + 1]
                cB = coefB[:, d, l, hh:hh + 1]
                cC = coefC[:, d, l, hh:hh + 1]
                for half in range(2):
                    for g2 in range(2):
                        g = 2 * half + g2

                        def evac(bank, g2=g2):
                            S.op("act", lambda: nc.scalar.activation(out=T_sb[:, 512 * g2:512 * (g2 + 1)], in_=ps[:, bank, :], func=AF.Tanh, scale=0.5),
                                 [PS[bank]], [bf("T_sb")])
                        fm_proj(wi, lambda k, d=d: W[:, k, 2 + d, :], g, evac)
                    L = Lb_sb[:, 16:1040]
                    S.op("act", lambda: nc.scalar.activation(out=L, in_=T_sb, func=AF.Ln, scale=cA, bias=cB), [bf("T_sb"), bf("coef")], [bf("Lb")])
                    E3 = E_sb.rearrange("p (n t) -> p n t", t=32)
                    L3 = L.rearrange("p (n t) -> p n t", t=32)
                    Ah = AZ[:, d, 0, 32 * half:32 * (half + 1)]
                    Zh = AZ[:, d, 1, 32 * half:32 * (half + 1)]
                    if d == 0:
                        S.op("dve", lambda: nc.vector.tensor_tensor_scan(out=E_sb, data0=smask[:], data1=L, initial=0.0, op0=ALU.mult, op1=ALU.add),
                             [bf("Lb"), bf("smask")], [bf("E")])
                        S.op("dve", lambda: nc.vector.tensor_copy(out=Ah, in_=E3[:, :, 15]), [bf("E")], [bf("AZ")])
                        S.op("dve", lambda: nc.vector.tensor_tensor(out=Zh, in0=E3[:, :, 31], in1=E3[:, :, 15], op=ALU.subtract), [bf("E")], [bf("AZ")])
                        S.op("dve", lambda: nc.vector.tensor_tensor(out=E3, in0=E3, in1=Ah.unsqueeze(2).to_broadcast([128, 32, 32]), op=ALU.subtract),
                             [bf("E"), bf("AZ")], [bf("E")])
                    else:
                        S.op("dve", lambda: nc.vector.tensor_tensor_scan(out=E_sb, data0=Lb_sb[:, 15:1039], data1=smask[:], initial=0.0, op0=ALU.add, op1=ALU.mult),
                             [bf("Lb"), bf("smask")], [bf("E")])
                        S.op("dve", lambda: nc.vector.tensor_copy(out=Zh, in_=E3[:, :, 16]), [bf("E")], [bf("AZ")])
                        S.op("dve", lambda: nc.vector.tensor_tensor(out=Ah, in0=E3[:, :, 31], in1=L3[:, :, 31], op=ALU.add), [bf("E"), bf("Lb")], [bf("AZ")])
                        S.op("dve", lambda: nc.vector.tensor_tensor(out=Ah, in0=Ah, in1=Zh, op=ALU.subtract), [bf("AZ")], [bf("AZ")])
                        S.op("dve", lambda: nc.vector.tensor_tensor(out=E3, in0=Zh.unsqueeze(2).to_broadcast([128, 32, 32]), in1=E3, op=ALU.subtract),
                             [bf("E"), bf("AZ")], [bf("E")])
                    hs = slice(1024 * half, 1024 * (half + 1))
                    S.op("act", lambda: nc.scalar.activation(out=L, in_=E_sb, func=AF.Exp), [bf("E")], [bf("Lb")])
                    S.op("dve", lambda: nc.vector.scalar_tensor_tensor(out=qa[d][:, hs], in0=q_sb[:, hs], scalar=float(2.0 ** -20), in1=L, op0=ALU.mult, op1=ALU.mult),
                         [bf("q_sb"), bf("Lb")], [bf(f"qa{d}")])
                    S.op("act", lambda: nc.scalar.activation(out=L, in_=E_sb, func=AF.Exp, scale=-1.0), [bf("E")], [bf("Lb")])
                    S.op("dve", lambda: nc.vector.tensor_scalar(out=T_sb, in0=T_sb, scalar1=cC, scalar2=cA, op0=ALU.mult, op1=ALU.add), [bf("T_sb"), bf("coef")], [bf("T_sb")])
                    S.op("dve", lambda: nc.vector.tensor_tensor(out=kab[d][:, hs], in0=T_sb, in1=L, op=ALU.mult), [bf("T_sb"), bf("Lb")], [bf(f"ka{d}")])
                ck(3.3)
                A_ = AZ[:, d, 0, :]
                Z_ = AZ[:, d, 1, :]
                if d == 0:
                    S.op("dve", lambda: nc.vector.tensor_tensor(out=c_sb[:, d, 1:64], in0=A_[:, 1:64], in1=Z_[:, 0:63], op=ALU.add), [bf("AZ")], [bf("c_sb")])
                    S.op("act", lambda: nc.scalar.activation(out=c_sb[:, d, 1:64], in_=c_sb[:, d, 1:64], func=AF.Exp), [bf("c_sb")], [bf("c_sb")])
                else:
                    S.op("dve", lambda: nc.vector.tensor_tensor(out=c_sb[:, d, 0:63], in0=A_[:, 0:63], in1=Z_[:, 1:64], op=ALU.add), [bf("AZ")], [bf("c_sb")])
                    S.op("act", lambda: nc.scalar.activation(out=c_sb[:, d, 0:63], in_=c_sb[:, d, 0:63], func=AF.Exp), [bf("c_sb")], [bf("c_sb")])
            for d in range(2):
                ck(3.4)
                for g in range(4):
                    bank = nxt("tm", [2, 3])
                    for jj in range(4):
                        j = 4 * g + jj
                        S.op("pe", lambda j=j, jj=jj: nc.tensor.matmul(ps[:, bank, 128 * jj:128 * (jj + 1)], lhsT=kab[d][:, 128 * j:128 * (j + 1)], rhs=ident_b[:], start=True, stop=True),
                             [bf(f"ka{d}"), bf("ident_b")], [PS[bank]])
                    S.op("dve", lambda g=g: nc.vector.tensor_copy(out=kaT[:, 4 * g:4 * g + 4, :].rearrange("p j d -> p (j d)"), in_=ps[:, bank, :]), [PS[bank]], [bf("kaT")])
                    S.op("act", lambda g=g: nc.scalar.copy(out=kaTz[64:128, 4 * g:4 * g + 4, :].rearrange("p j d -> p (j d)"), in_=ps[64:128, bank, :]), [PS[bank]], [bf("T_sb")])
                    if g == 3:
                        S.op("dve", lambda: nc.vector.memset(kaTz[64:96, :, :], 0.0), [], [bf("T_sb")])
                ck(3.5)
                tiles = range(16) if d == 0 else range(15, -1, -1)
                first = True
                rcur = 0
                for j in tiles:
                    bS = nxt("x", [6, 7])
                    S.op("pe", lambda j=j: nc.tensor.matmul(ps[:, bS, 0:128], lhsT=kab[d][:, 128 * j:128 * (j + 1)], rhs=qa[d][:, 128 * j:128 * (j + 1)], start=True, stop=True),
                         [bf(f"ka{d}"), bf(f"qa{d}")], [PS[bS]])
                    si = nxt(f"scT{d}", [0, 1])
                    S.op("dve", lambda: nc.vector.copy_predicated(out=scT[d][si], mask=tri[:, d, :], data=ps[:, bS, 0:128]), [PS[bS], bf("tri")], [bf(f"scT{d}{si}")])
                    bO = nxt("o", [4, 5])
                    S.op("pe", lambda j=j: nc.tensor.matmul(ps[:, bO, 0:128], lhsT=v_sb[:, j, :], rhs=scT[d][si], start=True, stop=False),
                         [bf(f"scT{d}{si}"), bf("v_sb")], [PS[bO]])
                    clist = list(range(4) if d == 0 else range(3, -1, -1))
                    for ci, c in enumerate(clist):
                        n = 4 * j + c
                        bP = nxt("x", [6, 7])
                        if c < 3:
                            S.op("pe", lambda j=j, c=c: nc.tensor.matmul(ps[:, bP, 128:256], lhsT=kaT[32 * c:32 * (c + 1), j, :], rhs=v_sb[32 * c:32 * (c + 1), j, :], start=True, stop=True),
                                 [bf("kaT"), bf("v_sb")], [PS[bP]])
                        else:
                            S.op("pe", lambda j=j: nc.tensor.matmul(ps[:, bP, 128:256], lhsT=kaTz[64:128, j, :], rhs=v_sb[64:128, j, :], start=True, stop=True),
                                 [bf("T_sb"), bf("v_sb")], [PS[bP]])
                        if first:
                            S.op("dve", lambda: nc.vector.tensor_copy(out=R_sb[rcur], in_=ps[:, bP, 128:256]), [PS[bP]], [bf(f"R{rcur}")])
                            first = False
                            continue
                        sbi = nxt("Sb", [0, 1])
                        S.op("act", lambda n=n: nc.scalar.activation(out=Sb_sb[sbi], in_=R_sb[rcur], func=AF.Copy, scale=c_sb[:, d, n:n + 1]),
                             [bf(f"R{rcur}"), bf("c_sb")], [bf(f"Sb{sbi}")])
                        last = (ci == 3)
                        S.op("pe", lambda n=n, c=c, last=last: nc.tensor.matmul(ps[:, bO, 32 * c:32 * (c + 1)], lhsT=Sb_sb[sbi], rhs=qa[d][:, 32 * n:32 * (n + 1)],
                                                                               start=False, stop=last), [bf(f"qa{d}"), bf(f"Sb{sbi}")], [PS[bO]])
                        rn = 1 - rcur
                        S.op("dve", lambda n=n, rn=rn, rc=rcur: nc.vector.scalar_tensor_tensor(out=R_sb[rn], in0=R_sb[rc], scalar=c_sb[:, d, n:n + 1], in1=ps[:, bP, 128:256],
                                                                                              op0=ALU.mult, op1=ALU.add), [bf(f"R{rcur}"), bf("c_sb"), PS[bP]], [bf(f"R{rn}")])
                        rcur = rn
                    if d == 0:
                        S.op("act", lambda j=j: nc.scalar.activation(out=o_all[:, 128 * j:128 * (j + 1)], in_=ps[:, bO, 0:128], func=AF.Copy, scale=float(2.0 ** 20)),
                             [PS[bO]], [bf("q_sb")])
                    else:
                        S.op("dve", lambda j=j: nc.vector.scalar_tensor_tensor(out=o_all[:, 128 * j:128 * (j + 1)], in0=ps[:, bO, 0:128], scalar=float(2.0 ** 20),
                                                                               in1=o_all[:, 128 * j:128 * (j + 1)], op0=ALU.mult, op1=ALU.add),
                             [PS[bO], bf("q_sb")], [bf("q_sb")])
            for g in range(4):
                def evac(bank, g=g):
                    t = E_sb[:, 512 * (g % 2):512 * (g % 2 + 1)]
                    S.op("act", lambda: nc.scalar.activation(out=t, in_=ps[:, bank, :], func=AF.Tanh, scale=0.5), [PS[bank]], [bf("E")])
                    S.op("dve", lambda: nc.vector.scalar_tensor_tensor(out=gsg[:, 512 * g:512 * (g + 1)], in0=t, scalar=1.0, in1=ps[:, bank, :], op0=ALU.add, op1=ALU.mult),
                         [PS[bank], bf("E")], [bf("ka0")])
                fm_proj(wi, lambda k: W[:, k, 4, :], g, evac)
            ck(3.7)
            o2 = arena[:, 2048:4096]
            S.op("act", lambda: nc.scalar.activation(out=o2, in_=o_all, func=AF.Square), [bf("q_sb")], TLE)
            for g in range(4):
                bank = nxt("fm", [0, 1])
                S.op("pe", lambda g=g: nc.tensor.matmul(ps[:, bank, :], lhsT=ones_f[:], rhs=o2[:, 512 * g:512 * (g + 1)], start=True, stop=True), TLE + [bf("ones")], [PS[bank]])
                S.op("act", lambda g=g: nc.scalar.activation(out=o2[:, 512 * g:512 * (g + 1)], in_=ps[:, bank, :], func=AF.Ln, bias=eps_ln[:, 1:2], scale=1.0 / 128.0),
                     [PS[bank], bf("eps")], TLE)
            S.op("act", lambda: nc.scalar.activation(out=o2, in_=o2, func=AF.Exp, scale=-0.5), TLE, TLE)
            S.op("dve", lambda: nc.vector.scalar_tensor_tensor(out=o2, in0=o_all, scalar=gnh_col[:, 0:1], in1=o2, op0=ALU.mult, op1=ALU.mult),
                 [bf("q_sb"), bf("gnh")] + TLE, TLE)
            S.op("dve", lambda: nc.vector.tensor_tensor(out=yT[:, slot, :], in0=o2, in1=gsg, op=ALU.mult), TLE + [bf("ka0")], [ByT[slot]])

        def emit_na_pair(l, p, slot):
            wi, W = load_w_na(l, p)
            for hh in range(2):
                S.dma_load("pool", bf(f"bt{hh}"), [(bt[hh], nab_d[l, 2 * p + hh])])
            for g in range(4):
                def evq(bank, g=g):
                    S.op("act", lambda: nc.scalar.activation(out=QT[:, 512 * g:512 * (g + 1)], in_=ps[:, bank, :], func=AF.Copy, scale=0.125), [PS[bank]], [bf("QT")])
                fm_proj(wi, lambda k: W[:, k, 0, :], g, evq)

                def evk(bank, g=g):
                    S.op("dve", lambda: nc.vector.tensor_copy(out=KT[:, 512 * g:512 * (g + 1)], in_=ps[:, bank, :]), [PS[bank]], [bf("KT")])
                fm_proj(wi, lambda k: W[:, k, 1, :], g, evk)
            for j in range(16):
                def evac(bank, j=j):
                    S.op("act", lambda: nc.scalar.copy(out=Vaug[:, j, :, 0:64], in_=ps[:, bank, 0:128].rearrange("p (h e) -> p h e", h=2)), [PS[bank]], [bf("Vaug")])
                    t = ytm[j % 2]
                    S.op("act", lambda: nc.scalar.activation(out=t, in_=ps[:, bank, 128:256], func=AF.Tanh, scale=0.5), [PS[bank]], [bf(f"ytm{j % 2}")])
                    S.op("dve", lambda: nc.vector.scalar_tensor_tensor(out=gsn[:, j, :], in0=t, scalar=1.0, in1=ps[:, bank, 128:256], op0=ALU.add, op1=ALU.mult),
                         [PS[bank], bf(f"ytm{j % 2}")], [bf("gsn")])
                tm_proj(wi, lambda k: W[:, k, 2:4, :], j, 256, evac)
            for a in range(16):
                yt = ytm[a % 2]
                for hh in range(2):
                    hp = 64 * hh
                    ms = NA_MLIST[a]
                    nb = len(ms)
                    bA, bB = nxt("sT", [(4, 5), (6, 7)])
                    for bi, (m, ti) in enumerate(ms):
                        bank, col = (bA, 128 * bi) if bi < 4 else (bB, 0)
                        S.op("pe", lambda m=m, bank=bank, col=col: nc.tensor.matmul(ps[:, bank, col:col + 128], lhsT=KT[hp:hp + 64, 128 * m:128 * (m + 1)],
                                                                                  rhs=QT[hp:hp + 64, 128 * a:128 * (a + 1)], start=True, stop=False),
                             [bf("KT"), bf("QT")], [PS[bank]])
                        S.op("pe", lambda ti=ti, bank=bank, col=col: nc.tensor.matmul(ps[:, bank, col:col + 128], lhsT=ident_b[:], rhs=bt[hh][:, 128 * ti:128 * (ti + 1)],
                                                                                    start=False, stop=True), [bf(f"bt{hh}"), bf("ident_b")], [PS[bank]])
                    pi = nxt("PT", [0, 1])
                    n4 = min(nb, 4)
                    S.op("act", lambda n4=n4: nc.scalar.activation(out=PTs[pi][:, 0:128 * n4], in_=ps[:, bA, 0:128 * n4], func=AF.Exp), [PS[bA]], [bf(f"PT{pi}")])
                    if nb == 5:
                        S.op("act", lambda: nc.scalar.activation(out=PTs[pi][:, 512:640], in_=ps[:, bB, 0:128], func=AF.Exp), [PS[bB]], [bf(f"PT{pi}")])
                    bV = nxt("tm", [2, 3])
                    for bi, (m, ti) in enumerate(ms):
                        S.op("pe", lambda bi=bi, m=m: nc.tensor.matmul(ps[:, bV, 0:65], lhsT=PTs[pi][:, 128 * bi:128 * (bi + 1)], rhs=Vaug[:, m, hh, :],
                                                                     start=(bi == 0), stop=(bi == nb - 1)), [bf(f"PT{pi}"), bf("Vaug")], [PS[bV]])
                    S.op("dve", lambda: nc.vector.reciprocal(out=rden[:, hh:hh + 1], in_=ps[:, bV, 64:65]), [PS[bV]], [bf("rden")])
                    S.op("dve", lambda: nc.vector.scalar_tensor_tensor(out=yt[:, hp:hp + 64], in0=ps[:, bV, 0:64], scalar=rden[:, hh:hh + 1], in1=gsn[:, a, hp:hp + 64],
                                                                       op0=ALU.mult, op1=ALU.mult), [PS[bV], bf("rden"), bf("gsn")], [bf(f"ytm{a % 2}")])
                bank = nxt("fm", [0, 1])
                S.op("pe", lambda: nc.tensor.transpose(out=ps[:, bank, 0:128], in_=yt, identity=ident_f[:]), [bf(f"ytm{a % 2}"), bf("ident_f")], [PS[bank]])
                S.op("act", lambda a=a: nc.scalar.activation(out=yT[:, slot, 128 * a:128 * (a + 1)], in_=ps[:, bank, 0:128], func=AF.Copy, scale=0.5), [PS[bank]], [ByT[slot]])

        def emit_outproj_round(l, r):
            for hh in range(2):
                wi = nxt("w", [0, 1])
                wv = wbuf[wi][:, 0:2048].rearrange("p (k c) -> p k c", c=512)
                S.dma_load("sp", Bw[wi], [(wv, wout_d[l, 512 * r:512 * (r + 1), 512 * hh:512 * (hh + 1)].rearrange("(k p) c -> p k c", p=128))])
                S.op("dve", lambda hh=hh, wv=wv: nc.vector.tensor_tensor(out=wo_sb[:, :, 512 * hh:512 * (hh + 1)], in0=wv,
                                                                       in1=gate_bc[:, 512 * hh:512 * (hh + 1)].unsqueeze(1).to_broadcast([128, 4, 512]), op=ALU.mult),
                     [Bw[wi], bf("gate_bc")], [bf("wo_sb")])
            for j in range(16):
                for hh in range(2):
                    bank = nxt("fm", [0, 1])
                    for k in range(4):
                        S.op("pe", lambda k=k, hh=hh: nc.tensor.matmul(ps[:, bank, :], lhsT=yT[:, k, 128 * j:128 * (j + 1)], rhs=wo_sb[:, k, 512 * hh:512 * (hh + 1)],
                                                                     start=(k == 0), stop=(k == 3)), [ByT[k], bf("wo_sb")], [PS[bank]])
                    xs = x_sb[:, j, 512 * hh:512 * (hh + 1)]
                    if r == 0:
                        S.op("dve", lambda xs=xs: nc.vector.scalar_tensor_tensor(out=xs, in0=xs, scalar=float(ALPHA), in1=ps[:, bank, :], op0=ALU.mult, op1=ALU.add),
                             [PS[bank], Bx[j]], [Bx[j]])
                    else:
                        S.op("dve", lambda xs=xs: nc.vector.tensor_tensor(out=xs, in0=ps[:, bank, :], in1=xs, op=ALU.add), [PS[bank], Bx[j]], [Bx[j]])

        def emit_ln2(l):
            S.dma_load("sp", bf("lng_bc"), [(lng_bc, lng_d[l:l + 1, :].to_broadcast([128, 1024]))])
            S.dma_load("sp", bf("lnb_bc"), [(lnb_bc, lnb_d[l:l + 1, :].to_broadcast([128, 1024]))])
            ln_stats()
            for j in range(16):
                t = xn4[:, j % 2, :]
                S.op("dve", lambda j=j, t=t: nc.vector.tensor_scalar(out=t, in0=x_sb[:, j, :], scalar1=mv[:, j, 0:1], scalar2=rstd[:, j:j + 1], op0=ALU.subtract, op1=ALU.mult),
                     [Bx[j], bf("mv"), bf("rstd")], [bf(f"xn{j % 2}")])
                S.op("pool", lambda t=t: nc.gpsimd.tensor_tensor(out=t, in0=t, in1=lng_bc, op=ALU.mult), [bf(f"xn{j % 2}"), bf("lng_bc")], [bf(f"xn{j % 2}")])
                S.op("dve", lambda j=j, t=t: nc.vector.tensor_tensor(out=x_sb[:, j, :], in0=t, in1=lnb_bc, op=ALU.add), [bf(f"xn{j % 2}"), bf("lnb_bc")], [Bx[j]])

        import os as _os
        STOP = float(_os.environ.get("KSTOP", "99"))

        class _Stop(Exception):
            pass

        def ck(level):
            if STOP <= level:
                raise _Stop()

        try:
            ck(0)
            for g in range(12):
                emit_mod_group(0, g)
            emit_mod_finish(0)
            ck(1)
            for l in range(NL):
                S.barrier()
                emit_gate_bc(l)
                ck(2)
                emit_ln1(l)
                ck(3)
                S.barrier()
                for d in range(2):
                    for i in range(2):
                        S.op("dve", lambda d=d, i=i: nc.vector.memset(scT[d][i], 0.0), [], [bf(f"scT{d}{i}")])
                S.op("dve", lambda: nc.vector.memset(Lb_sb[:, 0:16], 0.0), [], [bf("Lb")])
                for hh in range(8):
                    emit_hgrn_head(l, hh, hh % 4)
                    ck(4)
                    if l + 1 < NL:
                        emit_mod_group(l + 1, hh)
                    if hh % 4 == 3:
                        emit_outproj_round(l, hh // 4)
                        ck(5)
                S.barrier()
                S.op("dve", lambda: nc.vector.memset(Vaug[:, :, :, 64:65], 1.0), [], [bf("Vaug")])
                for p in range(8):
                    emit_na_pair(l, p, p % 4)
                    ck(6)
                    if l + 1 < NL and p < 4:
                        emit_mod_group(l + 1, 8 + p)
                    if p % 4 == 3:
                        emit_outproj_round(l, 2 + p // 4)
                S.barrier()
                if l + 1 < NL:
                    emit_mod_finish(l + 1)
                ck(7)
                emit_ln2(l)
        except _Stop:
            pass
        S.barrier()
        for g in range(4):
            S.dma_store("sp", [Bx[j] for j in range(4 * g, 4 * g + 4)],
                        [(y_d[512 * g:512 * (g + 1), :].rearrange("(j p) d -> p j d", p=128), x_sb[:, 4 * g:4 * g + 4, :])])
        S.finish()
    return nc


_CONST = {}


def _consts():
    if not _CONST:
        ident = np.eye(128, dtype=np.float32)
        smask = np.ones((128, 1024), np.float32)
        smask[:, 0::32] = 0.0
        s = np.arange(128)[:, None]
        t = np.arange(128)[None, :]
        same = (s // 32) == (t // 32)
        tri = np.stack([(same & (s <= t)), (same & (s >= t))]).astype(np.uint32)
        _CONST.update(ident=ident, smask=smask, tri=tri)
    return _CONST


def make_in_maps(x, c, ada_w, ada_b, w_in, lb_logits, hg_norm_g, rpb, w_out, ln_g, ln_b, n_cores=8):
    cst = _consts()
    f = lambda a: np.ascontiguousarray(np.asarray(a, dtype=np.float32))
    x, c, ada_w, ada_b, w_in = f(x), f(c), f(ada_w), f(ada_b), f(w_in)
    lb_logits, hg_norm_g, rpb, w_out, ln_g, ln_b = f(lb_logits), f(hg_norm_g), f(rpb), f(w_out), f(ln_g), f(ln_b)
    nab = host_na_bias(rpb)
    shared = dict(ada_w=ada_w, ada_b=ada_b.reshape(96, 128), w_in=w_in, lbl=lb_logits.reshape(64, 128), gn=hg_norm_g,
                  nab=nab, w_out=w_out, ln_g=ln_g, ln_b=ln_b, ident=cst["ident"], smask=cst["smask"], tri=cst["tri"])
    maps = []
    for b in range(n_cores):
        m = dict(shared)
        m["x"] = np.ascontiguousarray(x[b])
        m["c"] = np.ascontiguousarray(c[b].reshape(8, 128))
        maps.append(m)
    return maps


_NC = {}


def kernel(x, c, ada_w, ada_b, w_in, lb_logits, hg_norm_g, rpb, w_out, ln_g, ln_b):
    if "nc" not in _NC:
        _NC["nc"] = build(DEPTH)
    maps = make_in_maps(x, c, ada_w, ada_b, w_in, lb_logits, hg_norm_g, rpb, w_out, ln_g, ln_b, 8)
    res = run_bass_kernel_spmd(_NC["nc"], maps, core_ids=list(range(8)))
    return np.stack([np.asarray(r["y"], dtype=np.float32) for r in res.results], axis=0)
```
